# Optimizing a Trainium2 kernel written in Bass

```python
import jax, jax.numpy as jnp
from jax import lax
import numpy as np

D_MODEL = 4096
BATCH = 1
SEQ = 8192
DEPTH = 1

CHUNK = 64

FOX_HEAD_DIM = 128
FOX_WIDTH = D_MODEL // 2
FOX_HEADS = FOX_WIDTH // FOX_HEAD_DIM
Q_BLOCK = 128

POOL_WINDOWS = (2, 4, 8, 16)
POOL_GROUPS = len(POOL_WINDOWS)
POOL_WIDTH = D_MODEL // 2
POOL_GROUP_DIM = POOL_WIDTH // POOL_GROUPS
POOL_OUT_GROUP = D_MODEL // POOL_GROUPS

N_BRANCHES = 2
IN_COLS = 3 * FOX_WIDTH + FOX_HEADS + POOL_WIDTH + N_BRANCHES * D_MODEL

FFN_HIDDEN = -(-(8 * D_MODEL) // (3 * 256)) * 256

ALPHA = (2 * DEPTH) ** 0.25
BETA = (8 * DEPTH) ** -0.25
LN_EPS = 1e-5

kernel_name = "hybrid_fox_pool_deepnorm_adaln_block"


def _layer_norm(x, gain=None, bias=None):
    xf = x.astype(jnp.float32)
    mu = jnp.mean(xf, axis=-1, keepdims=True)
    var = jnp.mean(jnp.square(xf - mu), axis=-1, keepdims=True)
    y = (xf - mu) * lax.rsqrt(var + LN_EPS)
    if gain is not None:
        y = y * gain.astype(jnp.float32) + bias.astype(jnp.float32)
    return y.astype(x.dtype)


def _fox_attention(q, k, v, fcum):
    B, S, H, Dh = q.shape
    nb = S // Q_BLOCK
    scale = Dh ** -0.5
    kh = k.transpose(0, 2, 1, 3)
    vh = v.transpose(0, 2, 1, 3)
    fk = fcum.transpose(0, 2, 1)
    qb = q.reshape(B, nb, Q_BLOCK, H, Dh).transpose(1, 0, 3, 2, 4)
    fq = fk.reshape(B, H, nb, Q_BLOCK).transpose(2, 0, 1, 3)
    k_pos = jnp.arange(S)

    def one_block(args):
        q_blk, fq_blk, blk = args
        q_pos = blk * Q_BLOCK + jnp.arange(Q_BLOCK)
        logits = jnp.einsum('bhqd,bhkd->bhqk', q_blk, kh).astype(jnp.float32) * scale
        logits = logits + (fq_blk[..., :, None] - fk[..., None, :])
        mask = k_pos[None, :] <= q_pos[:, None]
        logits = jnp.where(mask, logits, -jnp.inf)
        probs = jax.nn.softmax(logits, axis=-1)
        return jnp.einsum('bhqk,bhkd->bhqd', probs.astype(vh.dtype), vh)

    out = lax.map(one_block, (qb, fq, jnp.arange(nb)))
    return out.transpose(1, 0, 3, 2, 4).reshape(B, S, H * Dh)


def _multiscale_pool(p, w_pool, pool_scale):
    B, S, _ = p.shape
    pg = p.reshape(B, S, POOL_GROUPS, POOL_GROUP_DIM).astype(jnp.float32)
    csum = jnp.cumsum(pg, axis=1)
    t = jnp.arange(S)
    outs = []
    for g, w in enumerate(POOL_WINDOWS):
        cg = csum[:, :, g]
        lag = jnp.pad(cg, ((0, 0), (w, 0), (0, 0)))[:, :S]
        cnt = jnp.minimum(t + 1, w).astype(jnp.float32)[None, :, None]
        outs.append((cg - lag) / cnt - pg[:, :, g])
    pooled = jnp.stack(outs, axis=2).astype(p.dtype)
    y = jnp.einsum('bsgc,gce->bsge', pooled, w_pool).reshape(B, S, D_MODEL)
    return y * pool_scale


def _token_mixer(u, w_in, b_forget, w_attn_out, w_pool, pool_scale, w_out):
    B, S, _ = u.shape
    z = u @ w_in
    F, H = FOX_WIDTH, FOX_HEADS
    o1 = 3 * F
    o2 = o1 + H
    o3 = o2 + POOL_WIDTH
    o4 = o3 + D_MODEL
    q = z[..., 0:F].reshape(B, S, H, FOX_HEAD_DIM)
    k = z[..., F:2 * F].reshape(B, S, H, FOX_HEAD_DIM)
    v = z[..., 2 * F:o1].reshape(B, S, H, FOX_HEAD_DIM)
    f_logit = z[..., o1:o2].astype(jnp.float32) + b_forget.astype(jnp.float32)
    p = z[..., o2:o3]
    gate_a = jax.nn.sigmoid(z[..., o3:o4])
    gate_p = jax.nn.sigmoid(z[..., o4:])

    fcum = jnp.cumsum(jax.nn.log_sigmoid(f_logit), axis=1)
    y_a = _fox_attention(q, k, v, fcum) @ w_attn_out
    y_p = _multiscale_pool(p, w_pool, pool_scale)
    return (gate_a * y_a + gate_p * y_p) @ w_out


def _swiglu(u, w_gate_up, w_down):
    h = u @ w_gate_up
    g, up = h[..., :FFN_HIDDEN], h[..., FFN_HIDDEN:]
    return (jax.nn.silu(g) * up) @ w_down


def setup_inputs(seed: int = 0) -> dict:
    key = jax.random.key(seed)
    ks = jax.random.split(key, 17)
    L, D = DEPTH, D_MODEL
    nrm = lambda k, shape: jax.random.normal(k, shape, jnp.float32)
    x = nrm(ks[0], (BATCH, SEQ, D))
    c = nrm(ks[1], (BATCH, D))
    w_ada = nrm(ks[2], (L, D, 6 * D)) * (0.5 * D ** -0.5)
    b_ada = nrm(ks[3], (L, 6 * D)) * 0.02
    col_scale = jnp.ones((IN_COLS,), jnp.float32).at[2 * FOX_WIDTH:3 * FOX_WIDTH].set(BETA)
    w_in = nrm(ks[4], (L, D, IN_COLS)) * (D ** -0.5) * col_scale
    b_forget = (jnp.linspace(1.0, 6.0, FOX_HEADS, dtype=jnp.float32)[None, :]
                + 0.1 * nrm(ks[5], (L, FOX_HEADS)))
    w_attn_out = nrm(ks[6], (L, FOX_WIDTH, D)) * FOX_WIDTH ** -0.5
    w_pool = nrm(ks[7], (L, POOL_GROUPS, POOL_GROUP_DIM, POOL_OUT_GROUP)) * POOL_GROUP_DIM ** -0.5
    pool_scale = 1.0 + 0.1 * nrm(ks[8], (L, D))
    w_out = nrm(ks[9], (L, D, D)) * (D ** -0.5) * BETA
    ln1_g = 1.0 + 0.02 * nrm(ks[10], (L, D))
    ln1_b = 0.02 * nrm(ks[11], (L, D))
    w_gate_up = nrm(ks[12], (L, D, 2 * FFN_HIDDEN)) * D ** -0.5
    w_down = nrm(ks[13], (L, FFN_HIDDEN, D)) * (FFN_HIDDEN ** -0.5) * BETA
    ln2_g = 1.0 + 0.02 * nrm(ks[14], (L, D))
    ln2_b = 0.02 * nrm(ks[15], (L, D))
    return {"x": x, "c": c, "w_ada": w_ada, "b_ada": b_ada, "w_in": w_in,
            "b_forget": b_forget, "w_attn_out": w_attn_out, "w_pool": w_pool,
            "pool_scale": pool_scale, "w_out": w_out, "ln1_g": ln1_g, "ln1_b": ln1_b,
            "w_gate_up": w_gate_up, "w_down": w_down, "ln2_g": ln2_g, "ln2_b": ln2_b}


def reference(x, c, w_ada, b_ada, w_in, b_forget, w_attn_out, w_pool, pool_scale,
              w_out, ln1_g, ln1_b, w_gate_up, w_down, ln2_g, ln2_b):
    D = D_MODEL
    for l in range(DEPTH):
        mod = (jax.nn.silu(c) @ w_ada[l] + b_ada[l])[:, None, :]
        sh1, sc1, g1 = mod[..., 0:D], mod[..., D:2 * D], mod[..., 2 * D:3 * D]
        sh2, sc2, g2 = mod[..., 3 * D:4 * D], mod[..., 4 * D:5 * D], mod[..., 5 * D:]
        u = _layer_norm(x) * (1.0 + sc1) + sh1
        m = _token_mixer(u, w_in[l], b_forget[l], w_attn_out[l], w_pool[l],
                         pool_scale[l], w_out[l])
        x = _layer_norm(ALPHA * x + g1 * m, ln1_g[l], ln1_b[l])
        u = _layer_norm(x) * (1.0 + sc2) + sh2
        f = _swiglu(u, w_gate_up[l], w_down[l])
        x = _layer_norm(ALPHA * x + g2 * f, ln2_g[l], ln2_b[l])
    return x
```

```python
import contextlib
import numpy as np
import concourse.bass as bass
import concourse.mybir as mybir
from concourse.bass_utils import run_bass_kernel_spmd

F32 = mybir.dt.float32
BF16 = mybir.dt.bfloat16
AF = mybir.ActivationFunctionType
ALU = mybir.AluOpType
AX = mybir.AxisListType

NCORES = 8
D = 4096
KC = 32
SEQ = 8192
T = 1024
NB = 8
TH = 512
HEADS = 16
DH = 128
FW = 2048
HF = 11008
HC = 86
IN_COLS = 16400
C_Q, C_K, C_V, C_F, C_P, C_GA, C_GP = 0, 2048, 4096, 6144, 6160, 8208, 12304
ALPHA = 2.0 ** 0.25
LN_EPS = 1e-5
QSCALE = DH ** -0.5
WINDOWS = (2, 4, 8, 16)
BIAS_CLAMP = 60.0
EPOCH = 30000
DEBUG = False
STOP_AFTER = 4


def gblock(r, j):
    return 16 * (j // 2) + (r if j % 2 == 0 else 15 - r)


class Op:
    __slots__ = ("eng", "fn", "deps", "sig", "idx", "dma", "dcount", "inc")


class Prog:
    ENGS = ("pe", "act", "dve", "pool", "sp")

    def __init__(self, nc):
        self.nc = nc
        self.ops = []
        self.writers = {}
        self.readers = {}
        self.wfull = {}
        self.last_eng = {}
        self.last_dma = {}
        self.pending_barrier = {}
        self.epoch_deps = {}

    def op(self, eng, fn, reads=(), writes=(), pwrites=(), dma=None, inc=16):
        o = Op()
        o.eng, o.fn, o.dma, o.inc, o.sig, o.dcount = eng, fn, dma, inc, None, None
        deps = set()
        for k in reads:
            deps.update(self.writers.get(k, ()))
        for k in writes:
            deps.update(self.writers.get(k, ()))
            deps.update(self.readers.get(k, ()))
        for k in pwrites:
            rd = self.readers.get(k, ())
            if rd:
                ed = list(rd) + list(self.writers.get(k, ()))
                self.epoch_deps[k] = ed
                deps.update(ed)
            elif self.wfull.get(k, False):
                ed = list(self.writers.get(k, ()))
                self.epoch_deps[k] = ed
                deps.update(ed)
            else:
                deps.update(self.epoch_deps.get(k, ()))
        for k in reads:
            self.readers.setdefault(k, []).append(o)
        for k in writes:
            self.writers[k] = [o]
            self.readers[k] = []
            self.wfull[k] = True
            self.epoch_deps[k] = []
        for k in pwrites:
            if self.readers.get(k) or self.wfull.get(k, False):
                self.writers[k] = [o]
                self.readers[k] = []
            else:
                self.writers.setdefault(k, []).append(o)
            self.wfull[k] = False
        pb = self.pending_barrier.pop(eng, None)
        if pb:
            deps.update(pb)
        deps.discard(o)
        best = {}
        for d in deps:
            if d.eng == "pe" and eng == "pe" and d.dma is None and dma is None:
                continue
            gk = ("e", d.eng) if d.dma is None else ("d", d.dma)
            if gk not in best or d.idx > best[gk].idx:
                best[gk] = d
        o.deps = list(best.values())
        o.idx = len(self.ops)
        self.ops.append(o)
        if dma is None:
            self.last_eng[eng] = o
        else:
            self.last_dma[dma] = o
        return o

    def barrier(self):
        allp = list(self.last_eng.values()) + list(self.last_dma.values())
        for e in self.ENGS:
            self.pending_barrier[e] = list(allp)

    def emit(self):
        nc = self.nc
        need = set()
        for o in self.ops:
            for d in o.deps:
                need.add(d.idx)
        counters = {e: 0 for e in self.ENGS}
        dcounts = {}
        for o in self.ops:
            if o.dma is not None:
                c = dcounts.get(o.dma, 0) + o.inc
                dcounts[o.dma] = c
                o.dcount = c
            elif o.idx in need:
                o.sig = counters[o.eng]
                counters[o.eng] += 1
        stack = contextlib.ExitStack()
        esems = {}
        for e in self.ENGS:
            n = counters[e] // EPOCH + 1
            esems[e] = [stack.enter_context(nc.semaphore(f"c_{e}_{i}")) for i in range(n)]
        dsems = {}
        for i, k in enumerate(dcounts):
            dsems[k] = stack.enter_context(nc.semaphore(f"d{i}_{k}"))
        self.nsems = sum(len(v) for v in esems.values()) + len(dsems)
        by_eng = {e: [o for o in self.ops if o.eng == e] for e in self.ENGS}

        def run(engname, eng):
            waited = {}
            for o in by_eng[engname]:
                req = {}
                for d in o.deps:
                    if d.dma is not None:
                        sem, val = dsems[d.dma], d.dcount
                    else:
                        sem, val = esems[d.eng][d.sig // EPOCH], d.sig % EPOCH + 1
                    key = id(sem)
                    if val > req.get(key, (None, 0))[1]:
                        req[key] = (sem, val)
                for key, (sem, val) in req.items():
                    if waited.get(key, 0) >= val:
                        continue
                    waited[key] = val
                    eng.wait_ge(sem, val)
                ins = o.fn(eng)
                if o.dma is not None:
                    ins.then_inc(dsems[o.dma], o.inc)
                elif o.sig is not None:
                    ins.then_inc(esems[engname][o.sig // EPOCH], 1)
            return waited

        with stack:
            with nc.Block() as block:
                @block.tensor
                def _(e):
                    run("pe", e)

                @block.scalar
                def _(e):
                    run("act", e)

                @block.vector
                def _(e):
                    run("dve", e)

                @block.gpsimd
                def _(e):
                    run("pool", e)

                @block.sync
                def _(e):
                    w = run("sp", e)
                    for k, c in dcounts.items():
                        if w.get(id(dsems[k]), 0) < c:
                            e.wait_ge(dsems[k], c)


class Arena:
    BASE = 17664
    END = 228352

    def __init__(self, nc):
        self.nc = nc
        self.cur = self.BASE
        self.n = 0
        self.peak = 0

    def alloc(self, name, shape, dt):
        esz = 2 if dt == BF16 else 4
        size = esz
        for s in shape[1:]:
            size *= s
        size = (size + 63) // 64 * 64
        assert self.cur + size <= self.END, (name, self.cur, size)
        t = self.nc.alloc_sbuf_tensor_at(f"{name}_{self.n}", list(shape), dt, offset=self.cur)
        self.n += 1
        self.cur += size
        self.peak = max(self.peak, self.cur)
        return t

    def mark(self):
        return self.cur

    def release(self, m):
        self.cur = m


CO_IDENT = 0
CO_ONES = 128
CO_TRI = 256
CO_SEL = 384
CO_L = 512
CO_OHR = 576
CO_HALO = 584
CO_INVC = 712
CO_CT = 1224
CO_PST = 1256
CO_BADA = 1288
CO_BF = 1312
NCONST = 1328


def build_program():
    nc = bass.Bass("TRN2", target_bir_lowering=False)

    def din(name, shape, dt=F32):
        return nc.dram_tensor(name, list(shape), dt, kind="ExternalInput").ap()

    xs = din("xs", [T + 128, D])
    consts_d = din("consts", [128, NCONST])
    masks_d = din("masks", [128, 2 * 8 * 128])
    w_ada = din("w_ada_s", [D, 3072])
    if STOP_AFTER >= 0.6:
        w_in_a = din("w_in_a", [D, C_GA])
    if STOP_AFTER >= 3:
        w_in_g = din("w_in_g", [D, IN_COLS - C_GA])
        w_ao = din("w_attn_out", [FW, D])
        w_pl = din("w_pool", [2048, 1024])
        w_out = din("w_out", [D, D])
    if STOP_AFTER >= 4:
        w_gu = din("w_gate_up", [D, 2 * HF])
        w_dn = din("w_down", [HF, D])
        ln1g_d = din("ln1_g", [1, D])
        ln1b_d = din("ln1_b", [1, D])
        ln2g_d = din("ln2_g", [1, D])
        ln2b_d = din("ln2_b", [1, D])
    out_d = nc.dram_tensor("out", [T, D], F32, kind="ExternalOutput").ap()
    dbg = {}

    def dbg_out(name, shape, dt=F32):
        if DEBUG:
            dbg[name] = nc.dram_tensor("dbg_" + name, list(shape), dt, kind="ExternalOutput").ap()
            return dbg[name]
        return None

    def dump(dst, src2d, key, reads, ncols, step=4096):
        for i, c0 in enumerate(range(0, ncols, step)):
            c1 = min(ncols, c0 + step)
            P.op("sp", (lambda o, a: (lambda e: e.dma_start(out=o, in_=a)))(dst[:, c0:c1], src2d[:, c0:c1]), reads=reads, dma=f"{key}_{i % 4}")

    modb = nc.dram_tensor("modb", [128, 24], F32)
    modg = nc.dram_tensor("modg", [NCORES * 128, 24], F32)
    kb = [nc.dram_tensor(f"kb{g}", [1024, T], BF16) for g in range(2)]
    kg = [nc.dram_tensor(f"kg{g}", [NCORES * 1024, T], BF16) for g in range(2)]
    vb = [nc.dram_tensor(f"vb{g}", [T, 1024], BF16) for g in range(2)]
    vg = [nc.dram_tensor(f"vg{g}", [NCORES * T, 1024], BF16) for g in range(2)]
    lsb = nc.dram_tensor("lsb", [128, 128], F32)
    lsg = nc.dram_tensor("lsg", [NCORES * 128, 128], F32)
    ybuf = nc.dram_tensor("ybuf", [T, D], F32).ap()
    x1buf = nc.dram_tensor("x1buf", [T, D], F32).ap()
    y2buf = nc.dram_tensor("y2buf", [T, D], F32).ap()

    P = Prog(nc)
    A = Arena(nc)
    ps = [nc.alloc_psum_tensor(f"psb{i}", [128, 512], F32) for i in range(8)]
    PSK = [f"ps{i}" for i in range(8)]

    def dma(eng, out, in_, key, reads=(), writes=(), pwrites=()):
        return P.op(eng, lambda e: e.dma_start(out=out, in_=in_), reads=reads, writes=writes, pwrites=pwrites, dma=key)

    def mm(out, lhsT, rhs, start, stop, reads, writes=(), pwrites=(), skip=False):
        if skip:
            return P.op("pe", lambda e: e.matmul(out, lhsT=lhsT, rhs=rhs, start=start, stop=stop, skip_group_check=True),
                        reads=reads, writes=writes, pwrites=pwrites)
        return P.op("pe", lambda e: e.matmul(out, lhsT=lhsT, rhs=rhs, start=start, stop=stop),
                    reads=reads, writes=writes, pwrites=pwrites)

    def tr(out, in_, ident, reads, writes=(), pwrites=()):
        return P.op("pe", lambda e: e.transpose(out, in_, ident), reads=reads, writes=writes, pwrites=pwrites)

    def act(out, in_, func, reads, writes=(), pwrites=(), bias=None, scale=None):
        kw = {}
        if bias is not None:
            kw["bias"] = bias
        if scale is not None:
            kw["scale"] = scale
        return P.op("act", lambda e: e.activation(out=out, in_=in_, func=func, **kw), reads=reads, writes=writes, pwrites=pwrites)

    def tt(eng, out, in0, in1, op, reads, writes=(), pwrites=()):
        return P.op(eng, lambda e: e.tensor_tensor(out=out, in0=in0, in1=in1, op=op), reads=reads, writes=writes, pwrites=pwrites)

    def ts(eng, out, in0, s1, s2, op0, op1, reads, writes=(), pwrites=()):
        if s2 is None:
            return P.op(eng, lambda e: e.tensor_scalar(out=out, in0=in0, scalar1=s1, scalar2=None, op0=op0),
                        reads=reads, writes=writes, pwrites=pwrites)
        return P.op(eng, lambda e: e.tensor_scalar(out=out, in0=in0, scalar1=s1, scalar2=s2, op0=op0, op1=op1),
                    reads=reads, writes=writes, pwrites=pwrites)

    def stt(eng, out, in0, scalar, in1, op0, op1, reads, writes=(), pwrites=()):
        return P.op(eng, lambda e: e.scalar_tensor_tensor(out=out, in0=in0, scalar=scalar, in1=in1, op0=op0, op1=op1),
                    reads=reads, writes=writes, pwrites=pwrites)

    def cp(eng, out, in_, reads, writes=(), pwrites=()):
        if eng == "act":
            return P.op("act", lambda e: e.copy(out=out, in_=in_), reads=reads, writes=writes, pwrites=pwrites)
        return P.op(eng, lambda e: e.tensor_copy(out=out, in_=in_), reads=reads, writes=writes, pwrites=pwrites)

    def psbf(i):
        return ps[i][:].bitcast(BF16).rearrange("p (a b) -> p a b", a=8)

    cst = A.alloc("cst", [128, NCONST], F32)
    ident_f = cst[:, CO_IDENT:CO_IDENT + 128]
    ones_f = cst[:, CO_ONES:CO_ONES + 128]
    tri_f = cst[:, CO_TRI:CO_TRI + 128]
    sel_f = cst[:, CO_SEL:CO_SEL + 128]
    Lmat = cst[0:64, CO_L:CO_L + 64]
    ohr = cst[:, CO_OHR:CO_OHR + 8]
    halo_m = cst[:, CO_HALO:CO_HALO + 128]
    invc0 = cst[:, CO_INVC:CO_INVC + 512]
    cT = cst[:, CO_CT:CO_CT + 32]
    psT = cst[:, CO_PST:CO_PST + 32]
    badaT = cst[:, CO_BADA:CO_BADA + 24]
    bf_rep = cst[:, CO_BF:CO_BF + 16]
    cbf = A.alloc("cbf", [128, 256], BF16)
    ident_b = cbf[:, 0:128]
    ones_b = cbf[:, 128:256]
    masks = A.alloc("masks", [128, 2, 8, 128], BF16)
    modT = A.alloc("modT", [128, 192], F32)
    sc1p = A.alloc("sc1p", [128, 32], F32)
    sc2p = A.alloc("sc2p", [128, 32], F32)
    scT = A.alloc("scT", [128, 32], F32)
    lnsm = [dict(st=A.alloc("st", [128, 8, 6], F32), mv=A.alloc("mv", [128, 2], F32),
                 rstd=A.alloc("rstd", [128, 1], F32), nmr=A.alloc("nmr", [128, 1], F32)) for _ in range(2)]

    dma("sp", cst[:], consts_d, "cst", writes=["cst"])
    dma("pool", masks[:].rearrange("p a b c -> p (a b c)"), masks_d, "masks", writes=["masks"])
    cp("dve", cbf[:, 0:128], ident_f, reads=["cst"], pwrites=["cbf"])
    cp("dve", cbf[:, 128:256], ones_f, reads=["cst"], pwrites=["cbf"])

    m0 = A.mark()
    wad = [A.alloc("wad", [128, 3072], F32) for _ in range(4)]
    modl = A.alloc("modl", [128, 24], F32)
    act(scT[:], cT, AF.Silu, reads=["cst"], writes=["scT"])
    for k in range(KC):
        s = k % 4
        dma("sp", wad[s][:], w_ada[k * 128:(k + 1) * 128, :], f"wad{s}", writes=[f"wad{s}"])
        for j in range(24):
            mm(ps[0][:, j:j + 1], wad[s][:, j * 128:(j + 1) * 128], scT[:, k:k + 1], start=(k == 0 and j == 0),
               stop=(k == KC - 1), reads=[f"wad{s}", "scT"], pwrites=["ps0"], skip=True)
    tt("dve", modl[:], ps[0][:, 0:24], badaT, ALU.add, reads=["ps0", "cst"], writes=["modl"])
    dma("sp", modb.ap(), modl[:], "modb", reads=["modl"], writes=["modb"])
    P.op("pool", lambda e: e.collective_compute("AllGather", ALU.bypass, replica_groups=[list(range(NCORES))],
                                                ins=[modb.ap().opt()], outs=[modg.ap().opt()]),
         reads=["modb"], writes=["modg"], dma="cc_mod", inc=1)
    dma("sp", modT[:].rearrange("p (r j) -> p r j", r=NCORES), modg.ap().rearrange("(r p) j -> p r j", p=128),
        "modT", reads=["modg"], writes=["modT"])
    ts("dve", sc1p[:], modT[:, 32:64], 1.0, None, ALU.add, None, reads=["modT"], writes=["sc1p"])
    ts("dve", sc2p[:], modT[:, 128:160], 1.0, None, ALU.add, None, reads=["modT"], writes=["sc2p"])
    d_mod = dbg_out("modT", [128, 192])
    if DEBUG:
        dma("sp", d_mod, modT[:], "dbg_modT", reads=["modT"])
    P.barrier()
    A.release(m0)

    def ln_stats(xt, xkey, sm, smk):
        for i in range(8):
            P.op("dve", (lambda o, a: (lambda e: e.bn_stats(out=o, in_=a)))(sm["st"][:, i, :], xt[:, i * 512:(i + 1) * 512]),
                 reads=[xkey], pwrites=[smk + "st"])
        P.op("dve", lambda e: e.bn_aggr(out=sm["mv"][:], in_=sm["st"][:].rearrange("p a b -> p (a b)")),
             reads=[smk + "st"], writes=[smk + "mv"])
        act(sm["rstd"][:], sm["mv"][:, 1:2], AF.Sqrt, reads=[smk + "mv"], writes=[smk + "rstd"], bias=LN_EPS)
        P.op("dve", lambda e: e.reciprocal(out=sm["rstd"][:], in_=sm["rstd"][:]), reads=[smk + "rstd"], writes=[smk + "rstd"])
        stt("dve", sm["nmr"][:], sm["mv"][:, 0:1], -1.0, sm["rstd"][:], ALU.mult, ALU.mult,
            reads=[smk + "mv", smk + "rstd"], writes=[smk + "nmr"])

    def ln_to_uT(xt, xkey, xh, xhkey, par, dst_fn, dst_key, scp, shcol0, modkeys):
        sm = lnsm[par]
        smk = f"ln{par}"
        ln_stats(xt, xkey, sm, smk)
        act(xh[:], xt[:], AF.Identity, reads=[xkey, smk + "rstd", smk + "nmr"], writes=[xhkey],
            bias=sm["nmr"][:], scale=sm["rstd"][:])
        for g in range(4):
            bank = 2 * par + (g % 2)
            pv = psbf(bank)
            for i in range(8):
                c = g * 8 + i
                tr(pv[:, i, :], xh[:, c * 128:(c + 1) * 128], ident_b, reads=[xhkey, "cbf"], pwrites=[PSK[bank]])
            for i in range(8):
                c = g * 8 + i
                if g % 2 == 0:
                    ts("dve", dst_fn(c), pv[:, i, :], scp[:, c:c + 1], modT[:, shcol0 + c:shcol0 + c + 1], ALU.mult, ALU.add,
                       reads=[PSK[bank]] + modkeys, pwrites=[dst_key])
                else:
                    act(dst_fn(c), pv[:, i, :], AF.Identity, reads=[PSK[bank]] + modkeys, pwrites=[dst_key],
                        bias=modT[:, shcol0 + c:shcol0 + c + 1], scale=scp[:, c:c + 1])

    class Ring:
        def __init__(self, name, nslots, shape, base_view=None):
            self.t = [A.alloc(name, shape, BF16) for _ in range(nslots)]
            self.keys = [f"{name}{i}" for i in range(nslots)]
            self.n = nslots
            self.i = 0
            self.base_view = base_view or (lambda t: t[:])

        def view(self, base_view):
            r = Ring.__new__(Ring)
            r.t, r.keys, r.n, r.base_view, r.parent = self.t, self.keys, self.n, base_view, self
            return r

        def load(self, sel_fn, src):
            own = getattr(self, "parent", self)
            s = own.i % self.n
            own.i += 1
            v = sel_fn(self.base_view(self.t[s]))
            nk_ = v.shape[1]
            for i_, k0 in enumerate(range(0, nk_, 4)):
                k1 = min(nk_, k0 + 4)
                if i_ == 0:
                    first = dma("pool", v[:, k0:k1, :], src[:, k0:k1, :], self.keys[s], writes=[self.keys[s]])
                else:
                    o_ = dma("pool", v[:, k0:k1, :], src[:, k0:k1, :], self.keys[s], pwrites=[self.keys[s]])
                    o_.deps = [d_ for d_ in o_.deps if d_ is not first]
            return v, self.keys[s]

    def wsrc(W, k0, nk, c0, ncol):
        return W[k0 * 128:(k0 + nk) * 128, c0:c0 + ncol].rearrange("(k p) c -> p k c", p=128)

    def gemm_b(ring, W, col0, ncg, gw, nk, rhs_fn, rhs_keys, chunks, bank_fn, evac):
        for g0 in range(0, ncg, gw):
            g1 = min(gw, ncg - g0)
            wt, wk = ring.load(lambda v: v[:, 0:nk, 0:128 * g1], wsrc(W, 0, nk, col0 + g0 * 128, 128 * g1))
            for sub in range(g1):
                cg = g0 + sub
                for kc in range(nk):
                    for ci, (t0, n) in enumerate(chunks):
                        b = bank_fn(cg, ci)
                        mm(ps[b][:, 0:n], wt[:, kc, sub * 128:(sub + 1) * 128], rhs_fn(kc, t0, n), start=(kc == 0),
                           stop=(kc == nk - 1), reads=[wk] + rhs_keys, writes=[PSK[b]] if kc == 0 else (), pwrites=() if kc == 0 else [PSK[b]])
                evac(cg)

    def gemm_a(ring, W, nk, ksub, coltiles, lhs_fn, lhs_keys, tbs, bank_fn, evac):
        for cti, (c0, ncol) in enumerate(coltiles):
            for ks in range(0, nk, ksub):
                kn = min(ksub, nk - ks)
                wt, wk = ring.load(lambda v: v[:, 0:kn, 0:ncol], wsrc(W, ks, kn, c0, ncol))
                for ti, tb in enumerate(tbs):
                    b = bank_fn(cti, ti)
                    for kk in range(kn):
                        kc = ks + kk
                        mm(ps[b][:, 0:ncol], lhs_fn(kc, tb), wt[:, kk, 0:ncol], start=(kc == 0), stop=(kc == nk - 1),
                           reads=[wk] + lhs_keys, writes=[PSK[b]] if kc == 0 else (), pwrites=() if kc == 0 else [PSK[b]])
            for ti, tb in enumerate(tbs):
                evac(cti, ti, tb, bank_fn(cti, ti))

    if STOP_AFTER < 0.5:
        P.emit()
        return nc, dbg, P, A
    mP = A.mark()
    AP_ = A.alloc("AP", [128, 32, T], BF16)
    m1 = A.mark()
    uT = A.alloc("uT", [128, 32, T + 128], BF16)
    m2 = A.mark()
    xt2 = [A.alloc("xt", [128, D], F32) for _ in range(2)]
    xh2 = [A.alloc("xh", [128, D], BF16) for _ in range(2)]
    for blk in range(NB + 1):
        par = blk % 2
        dma("sp", xt2[par][:], xs[blk * 128:(blk + 1) * 128, :], f"xt{par}", writes=[f"xt{par}"])
        ln_to_uT(xt2[par], f"xt{par}", xh2[par], f"xh{par}", par,
                 (lambda c, blk=blk: uT[:, c, blk * 128:(blk + 1) * 128]), f"uT_b{blk}", sc1p, 0, ["modT", "sc1p"])
    uT_keys = [f"uT_b{b}" for b in range(NB + 1)]
    d_uT = dbg_out("uT", [128, 32 * (T + 128)], BF16)
    if DEBUG:
        dump(d_uT, uT[:].rearrange("p a b -> p (a b)"), "dbg_uT", uT_keys, 32 * (T + 128))
    if STOP_AFTER < 0.6:
        P.emit()
        return nc, dbg, P, A
    P.barrier()
    A.release(m2)
    ring = Ring("wr", 4, [128, 8 * 512])
    kst = [A.alloc("kst", [128, T], BF16) for _ in range(2)]
    vst = [A.alloc("vst", [128, 512], BF16) for _ in range(2)]
    f_tok = A.alloc("f_tok", [128, 8, 16], F32)
    ls_tok = A.alloc("ls_tok", [128, 8, 16], F32)
    pp = A.alloc("pp", [128, 8, 144], F32)
    sA = A.alloc("sA", [128, 8, 144], F32)
    sB = A.alloc("sB", [128, 8, 144], F32)
    ptmp = A.alloc("ptmp", [128, 128], F32)

    ring128 = ring.view(lambda t: t[:].rearrange("p (k c) -> p k c", c=128))
    ring512 = ring.view(lambda t: t[:].rearrange("p (k c) -> p k c", c=512))

    def rhs_uT(kc, t0, n):
        return uT[:, kc, t0:t0 + n]

    def evac_k(cg):
        h = cg
        s = h % 2
        b0, b1 = 4 * (cg % 2), 4 * (cg % 2) + 1
        cp("act", kst[s][:, 0:512], ps[b0][:, :], reads=[PSK[b0]], pwrites=[f"kst{s}"])
        cp("dve", kst[s][:, 512:1024], ps[b1][:, :], reads=[PSK[b1]], pwrites=[f"kst{s}"])
        dma("sp", kb[h // 8].ap()[(h % 8) * 128:(h % 8 + 1) * 128, :], kst[s][:], f"kst_st{s}", reads=[f"kst{s}"], pwrites=[f"kb{h // 8}"])

    gemm_b(ring128, w_in_a, C_K, 16, 1, KC, rhs_uT, uT_keys, [(0, 512), (512, 512)],
           lambda cg, ci: 4 * (cg % 2) + ci, evac_k)
    if STOP_AFTER < 0.7:
        P.emit()
        return nc, dbg, P, A
    for g_ in range(2):
        P.op("pool", (lambda a, b: (lambda e: e.collective_compute("AllGather", ALU.bypass, replica_groups=[list(range(NCORES))],
                                                                    ins=[a.ap().opt()], outs=[b.ap().opt()])))(kb[g_], kg[g_]),
             reads=[f"kb{g_}"], writes=[f"kg{g_}"], dma=f"cc_k{g_}", inc=1)

    if STOP_AFTER < 0.8:
        P.emit()
        return nc, dbg, P, A
    vcount = [0]

    def evac_v(cti, ti, tb, b):
        if cti < 4:
            s = vcount[0] % 2
            vcount[0] += 1
            eng = "act" if s == 0 else "dve"
            cp(eng, vst[s][:], ps[b][:, :], reads=[PSK[b]], writes=[f"vst{s}"])
            dma("sp", vb[cti // 2].ap()[tb * 128:(tb + 1) * 128, (cti % 2) * 512:(cti % 2 + 1) * 512], vst[s][:], f"vst_st{s}",
                reads=[f"vst{s}"], pwrites=[f"vb{cti // 2}"])
        else:
            cp("dve", f_tok[:, tb, :], ps[b][:, 0:16], reads=[PSK[b]], pwrites=["f_tok"])

    gemm_a(ring512, w_in_a, KC, 8, [(C_V + i * 512, 512) for i in range(4)] + [(C_F, 16)],
           (lambda kc, tb: uT[:, kc, tb * 128:(tb + 1) * 128]), uT_keys, list(range(8)),
           lambda cti, ti: ti, evac_v)
    for g_ in range(2):
        P.op("pool", (lambda a, b: (lambda e: e.collective_compute("AllGather", ALU.bypass, replica_groups=[list(range(NCORES))],
                                                                    ins=[a.ap().opt()], outs=[b.ap().opt()])))(vb[g_], vg[g_]),
             reads=[f"vb{g_}"], writes=[f"vg{g_}"], dma=f"cc_v{g_}", inc=1)

    for j in range(8):
        tt("dve", f_tok[:, j, :], f_tok[:, j, :], bf_rep, ALU.add, reads=["f_tok", "cst"], pwrites=["f_tok2"])
    ftf = f_tok[:].rearrange("p a b -> p (a b)")
    lsf = ls_tok[:].rearrange("p a b -> p (a b)")
    act(lsf, ftf, AF.Exp, reads=["f_tok2"], writes=["ls_e"], scale=-1.0)
    act(lsf, lsf, AF.Ln, reads=["ls_e"], writes=["ls_l"], bias=1.0)
    ts("dve", lsf, lsf, -1.0, None, ALU.mult, None, reads=["ls_l"], writes=["ls_tok"])
    dma("sp", lsb.ap(), lsf, "lsb", reads=["ls_tok"], writes=["lsb"])
    P.op("pool", lambda e: e.collective_compute("AllGather", ALU.bypass, replica_groups=[list(range(NCORES))],
                                                ins=[lsb.ap().opt()], outs=[lsg.ap().opt()]),
         reads=["lsb"], writes=["lsg"], dma="cc_ls", inc=1)

    if STOP_AFTER < 0.9:
        P.emit()
        return nc, dbg, P, A
    def evac_q(cg):
        b0, b1 = 4 * (cg % 2), 4 * (cg % 2) + 1
        cp("act", AP_[:, cg, 0:512], ps[b0][:, :], reads=[PSK[b0]], pwrites=[f"QA{cg}"])
        cp("dve", AP_[:, cg, 512:1024], ps[b1][:, :], reads=[PSK[b1]], pwrites=[f"QA{cg}"])

    gemm_b(ring128, w_in_a, C_Q, 16, 1, KC, rhs_uT, uT_keys, [(0, 512), (512, 512)],
           lambda cg, ci: 4 * (cg % 2) + ci, evac_q)

    halo3 = halo_m.rearrange("p (a b) -> p a b", a=8)

    def evac_p(cg):
        g = cg // 4
        w = WINDOWS[g]
        b0 = 3 * (cg % 2)
        cp("act", pp[:, 0:4, 16:144], ps[b0][:, :].rearrange("p (a b) -> p a b", a=4), reads=[PSK[b0]], pwrites=["pp"])
        cp("act", pp[:, 4:8, 16:144], ps[b0 + 1][:, :].rearrange("p (a b) -> p a b", a=4), reads=[PSK[b0 + 1]], pwrites=["pp"])
        tt("dve", pp[:, :, 0:16], ps[b0 + 2][:, 0:128].rearrange("p (a b) -> p a b", a=8), halo3, ALU.mult,
           reads=[PSK[b0 + 2], "cst"], pwrites=["pp"])
        src, skey = pp, "pp"
        bufs = [(sA, "sA"), (sB, "sB")]
        sh = 1
        bi = 0
        lo = 0
        while sh < w:
            dst, dkey = bufs[bi]
            lo = lo + sh
            tt("dve", dst[:, :, lo:144], src[:, :, lo:144], src[:, :, lo - sh:144 - sh], ALU.add, reads=[skey], writes=[dkey])
            src, skey = dst, dkey
            bi ^= 1
            sh *= 2
        stt("dve", AP_[:, 16 + cg, 128:1024].rearrange("p (a b) -> p a b", a=7), src[:, 1:8, 16:144], 1.0 / w,
            pp[:, 1:8, 16:144], ALU.mult, ALU.subtract, reads=[skey, "pp"], pwrites=[f"PL{cg}"])
        tt("dve", ptmp[:], src[:, 0, 16:144], invc0[:, g * 128:(g + 1) * 128], ALU.mult, reads=[skey, "cst"], writes=["ptmp"])
        tt("dve", AP_[:, 16 + cg, 0:128], ptmp[:], pp[:, 0, 16:144], ALU.subtract, reads=["ptmp", "pp"], pwrites=[f"PL{cg}"])

    gemm_b(ring128, w_in_a, C_P, 16, 1, KC, rhs_uT, uT_keys, [(0, 512), (512, 512), (1024, 128)],
           lambda cg, ci: 3 * (cg % 2) + ci, evac_p)
    QA_keys = [f"QA{h}" for h in range(16)]
    PL_keys = [f"PL{c}" for c in range(16)]
    d_AP1 = dbg_out("AP1", [128, 32 * T], BF16)
    if DEBUG:
        dump(d_AP1, AP_[:].rearrange("p a b -> p (a b)"), "dbg_AP1", QA_keys + PL_keys, 32 * T)

    if STOP_AFTER < 2:
        P.emit()
        return nc, dbg, P, A
    P.barrier()
    A.release(m1)
    ls_all = A.alloc("ls_all", [128, 64, 16], F32)
    F_all = A.alloc("F_all", [128, 64, 16], F32)
    R_rep = A.alloc("R_rep", [128, 64, 16], F32)
    R_loc = A.alloc("R_loc", [128, 8, 16], F32)
    Rtmp = A.alloc("Rtmp", [128, 64, 16], F32)
    totT = A.alloc("totT", [64, 16], F32)
    LT = A.alloc("LT", [64, 64, 16], F32)
    Bh = [A.alloc("Bh", [128, 64, 8], F32) for _ in range(2)]
    ktc = [A.alloc("ktc", [128, T], BF16) for _ in range(3)]
    vtc = [A.alloc("vtc", [128, 8, 128], BF16) for _ in range(3)]
    ptl = [A.alloc("pt", [128, T], BF16) for _ in range(4)]
    rec = A.alloc("rec", [128, T], F32)

    dma("sp", ls_all[:].rearrange("p (r j) h -> p r (j h)", r=8), lsg.ap().rearrange("(r p) c -> p r c", p=128),
        "ls_all", reads=["lsg"], writes=["ls_all"])
    lsa_f = ls_all[:].rearrange("p a b -> p (a b)")
    for hf in range(2):
        mm(ps[4 + hf][:, :], tri_f, lsa_f[:, hf * 512:(hf + 1) * 512], True, True, reads=["cst", "ls_all"], writes=[PSK[4 + hf]])
    for h in range(16):
        mm(ps[6][0:64, h:h + 1], ls_all[:, :, h], ones_f[:, 0:1], True, True, reads=["cst", "ls_all"], pwrites=[PSK[6]], skip=True)
    cp("dve", totT[:], ps[6][0:64, 0:16], reads=[PSK[6]], writes=["totT"])
    for h in range(16):
        ts("dve", LT[:, :, h], Lmat, totT[:, h:h + 1], None, ALU.mult, None, reads=["cst", "totT"], pwrites=["LT"])
    LT_f = LT[:].rearrange("p a b -> p (a b)")
    for hf in range(2):
        mm(ps[6 + hf][:, :], ones_f[0:64, :], LT_f[:, hf * 512:(hf + 1) * 512], True, True, reads=["cst", "LT", "totT"], writes=[PSK[6 + hf]])
    F_f = F_all[:].rearrange("p a b -> p (a b)")
    for hf in range(2):
        cp("act", F_f[:, hf * 512:(hf + 1) * 512], ps[4 + hf][:, :], reads=[PSK[4 + hf]], pwrites=["F_a"])
    for hf in range(2):
        tt("dve", F_f[:, hf * 512:(hf + 1) * 512], F_f[:, hf * 512:(hf + 1) * 512], ps[6 + hf][:, :], ALU.add,
           reads=["F_a", PSK[6 + hf]], pwrites=["F_all"])
    for hf in range(2):
        mm(ps[4 + hf][:, :], sel_f, F_f[:, hf * 512:(hf + 1) * 512], True, True, reads=["cst", "F_all"], writes=[PSK[4 + hf]])
    R_f = R_rep[:].rearrange("p a b -> p (a b)")
    for hf in range(2):
        cp("act", R_f[:, hf * 512:(hf + 1) * 512], ps[4 + hf][:, :], reads=[PSK[4 + hf]], pwrites=["R_rep"])
    for r in range(8):
        ts("dve", Rtmp[:, r * 8:(r + 1) * 8, :], R_rep[:, r * 8:(r + 1) * 8, :], ohr[:, r:r + 1], None, ALU.mult, None,
           reads=["R_rep", "cst"], pwrites=["Rtmp"])
    P.op("dve", lambda e: e.tensor_reduce(out=R_loc[:].rearrange("p a b -> p (a b)"),
                                          in_=Rtmp[:].rearrange("p (r j) h -> p (j h) r", r=8), axis=AX.X, op=ALU.add),
         reads=["Rtmp"], writes=["R_loc"])
    d_F = dbg_out("F_all", [128, 1024])
    d_R = dbg_out("R_loc", [128, 128])
    if DEBUG:
        dma("sp", d_F, F_f, "dbg_F", reads=["F_all"])
        dma("sp", d_R, R_loc[:].rearrange("p a b -> p (a b)"), "dbg_R", reads=["R_loc"])

    kv_i = [0]
    pt_i = [0]
    sb_i = [0]
    for h in range(HEADS):
        bh = Bh[h % 2]
        bhk = f"Bh{h % 2}"
        for j in range(8):
            ts("dve", bh[:, :, j], F_all[:, :, h], -1.0, R_loc[:, j, h:h + 1], ALU.mult, ALU.add,
               reads=["F_all", "R_loc"], pwrites=[bhk])
        bh_f = bh[:].rearrange("p a b -> p (a b)")
        ts("dve", bh_f, bh_f, BIAS_CLAMP, None, ALU.min, None, reads=[bhk], writes=[bhk])
        chunks = []
        for rp in range(8):
            for jp in range(8):
                q0 = jp * 128
                if jp < 4:
                    chunks.append((rp, jp, q0, 512 - q0, 0))
                    chunks.append((rp, jp, 512, 512, 1))
                else:
                    chunks.append((rp, jp, q0, 1024 - q0, 1))
        loaded = {}
        state = {}

        def qk(ci):
            rp, jp, q0, n, ob = chunks[ci]
            if rp not in loaded:
                s = kv_i[0] % 3
                kv_i[0] += 1
                hg, hl = h // 8, h % 8
                dma("sp", ktc[s][:], kg[hg].ap()[rp * 1024 + hl * 128: rp * 1024 + (hl + 1) * 128, :], f"ktc{s}", reads=[f"kg{hg}"], writes=[f"ktc{s}"])
                dma("sp", vtc[s][:], vg[hg].ap()[rp * T:(rp + 1) * T, hl * 128:(hl + 1) * 128].rearrange("(j p) d -> p j d", p=128),
                    f"vtc{s}", reads=[f"vg{hg}"], writes=[f"vtc{s}"])
                loaded[rp] = s
            s = loaded[rp]
            sb = 4 + sb_i[0] % 4
            sb_i[0] += 1
            if (rp, jp) not in state:
                state[(rp, jp)] = pt_i[0] % 4
                pt_i[0] += 1
            pts = state[(rp, jp)]
            mm(ps[sb][:, 0:n], ktc[s][:, jp * 128:(jp + 1) * 128], AP_[:, h, q0:q0 + n], True, True,
               reads=[f"ktc{s}", f"QA{h}"], writes=[PSK[sb]])
            kbi = rp * 8 + jp
            ptk = f"pt{pts}_{q0 // 512}"
            for qb in range(n // 128):
                j = (q0 + qb * 128) // 128
                act(ptl[pts][:, q0 + qb * 128:q0 + (qb + 1) * 128], ps[sb][:, qb * 128:(qb + 1) * 128], AF.Exp,
                    reads=[PSK[sb], bhk], pwrites=[ptk], bias=bh[:, kbi, j:j + 1], scale=QSCALE)
            if q0 == jp * 128:
                tt("dve", ptl[pts][:, q0:q0 + 128], ptl[pts][:, q0:q0 + 128], masks[:, jp % 2, rp, :], ALU.mult,
                   reads=[ptk, "masks"], writes=[ptk])

        def pv(ci):
            rp, jp, q0, n, ob = chunks[ci]
            s = loaded[rp]
            pts = state[(rp, jp)]
            ptk = f"pt{pts}_{q0 // 512}"
            first = (rp == 0 and jp == 0)
            lastc = (rp == 7 and ((ob == 0 and jp == 3) or (ob == 1 and jp == 7)))
            mm(ps[ob][:, q0 - ob * 512:q0 - ob * 512 + n], vtc[s][:, jp, :], ptl[pts][:, q0:q0 + n], first, lastc,
               reads=[f"vtc{s}", ptk], writes=[PSK[ob]] if first else (), pwrites=() if first else [PSK[ob]], skip=True)
            mm(ps[2 + ob][:, q0 - ob * 512:q0 - ob * 512 + n], ones_b, ptl[pts][:, q0:q0 + n], first, lastc,
               reads=["cbf", ptk], writes=[PSK[2 + ob]] if first else (), pwrites=() if first else [PSK[2 + ob]], skip=True)

        LAG = 3
        nch = len(chunks)
        for ci in range(nch + LAG):
            if ci < nch:
                qk(ci)
            if ci >= LAG:
                pv(ci - LAG)
        for ob in range(2):
            P.op("dve", (lambda o, a: (lambda e: e.reciprocal(out=o, in_=a)))(rec[:, ob * 512:(ob + 1) * 512], ps[2 + ob][:, :]),
                 reads=[PSK[2 + ob]], pwrites=["rec"])
        for ob in range(2):
            tt("dve", AP_[:, h, ob * 512:(ob + 1) * 512], ps[ob][:, :], rec[:, ob * 512:(ob + 1) * 512], ALU.mult,
               reads=[PSK[ob], "rec"], pwrites=[f"QA{h}"])
    d_AP2 = dbg_out("AP2", [128, 16 * T], BF16)
    if DEBUG:
        dump(d_AP2, AP_[:].rearrange("p a b -> p (a b)"), "dbg_AP2", QA_keys, 16 * T)

    if STOP_AFTER < 3:
        P.emit()
        return nc, dbg, P, A
    P.barrier()
    A.release(m1)
    G0 = A.alloc("G0", [128, 32, TH], BF16)
    m5 = A.mark()
    uTh = A.alloc("uTh", [128, 32, TH], BF16)
    mX = A.mark()

    for th in range(2):
        A.release(mX)
        xt5 = [A.alloc("xt5", [128, D], F32) for _ in range(2)]
        xh5 = [A.alloc("xh5", [128, D], BF16) for _ in range(2)]
        for bi in range(4):
            blk = th * 4 + bi
            par = bi % 2
            dma("sp", xt5[par][:], xs[blk * 128:(blk + 1) * 128, :], f"xt5{par}", writes=[f"xt5{par}"])
            ln_to_uT(xt5[par], f"xt5{par}", xh5[par], f"xh5{par}", par,
                     (lambda c, bi=bi: uTh[:, c, bi * 128:(bi + 1) * 128]), f"uTh_b{bi}", sc1p, 0, ["modT", "sc1p"])
        uTh_keys = [f"uTh_b{b}" for b in range(4)]
        P.barrier()
        A.release(mX)
        rga = Ring("wga", 4, [128, 32, 128])
        rwo = Ring("wwo", 2, [128, 16, 128])
        rwq = Ring("wwq", 2, [128, 4, 128])
        sg1 = [A.alloc("sg1", [128, TH], F32) for _ in range(2)]
        sg2 = [A.alloc("sg2", [128, TH], F32) for _ in range(2)]
        t0 = th * TH
        for dc in range(32):
            par = dc % 2
            bz = 4 * par
            g = dc // 8
            wa, wak = rga.load(lambda v: v, wsrc(w_in_g, 0, 32, dc * 128, 128))
            wp, wpk = rga.load(lambda v: v, wsrc(w_in_g, 0, 32, (C_GP - C_GA) + dc * 128, 128))
            wo, wok = rwo.load(lambda v: v, wsrc(w_ao, 0, 16, dc * 128, 128))
            wq, wqk = rwq.load(lambda v: v, wsrc(w_pl, g * 4, 4, (dc % 8) * 128, 128))
            for kc in range(32):
                mm(ps[bz][:, :], wa[:, kc, :], uTh[:, kc, :], kc == 0, kc == 31, reads=[wak] + uTh_keys,
                   writes=[PSK[bz]] if kc == 0 else (), pwrites=() if kc == 0 else [PSK[bz]])
            for kc in range(32):
                mm(ps[bz + 1][:, :], wp[:, kc, :], uTh[:, kc, :], kc == 0, kc == 31, reads=[wpk] + uTh_keys,
                   writes=[PSK[bz + 1]] if kc == 0 else (), pwrites=() if kc == 0 else [PSK[bz + 1]])
            for kc in range(16):
                mm(ps[bz + 2][:, :], wo[:, kc, :], AP_[:, kc, t0:t0 + TH], kc == 0, kc == 15, reads=[wok, f"QA{kc}"],
                   writes=[PSK[bz + 2]] if kc == 0 else (), pwrites=() if kc == 0 else [PSK[bz + 2]])
            for kc in range(4):
                mm(ps[bz + 3][:, :], wq[:, kc, :], AP_[:, 16 + g * 4 + kc, t0:t0 + TH], kc == 0, kc == 3,
                   reads=[wqk, f"PL{g * 4 + kc}"], writes=[PSK[bz + 3]] if kc == 0 else (), pwrites=() if kc == 0 else [PSK[bz + 3]])
            act(sg1[par][:], ps[bz][:, :], AF.Sigmoid, reads=[PSK[bz]], writes=[f"sg1{par}"])
            act(sg2[par][:], ps[bz + 1][:, :], AF.Sigmoid, reads=[PSK[bz + 1]], writes=[f"sg2{par}"])
            tt("dve", sg1[par][:], sg1[par][:], ps[bz + 2][:, :], ALU.mult, reads=[f"sg1{par}", PSK[bz + 2]], writes=[f"sg1{par}"])
            stt("dve", sg2[par][:], ps[bz + 3][:, :], psT[:, dc:dc + 1], sg2[par][:], ALU.mult, ALU.mult,
                reads=[f"sg2{par}", PSK[bz + 3], "cst"], writes=[f"sg2{par}"])
            if th == 0:
                gdst, gkey = G0[:, dc, :], f"G0_{dc}"
                extra = []
            else:
                gdst, gkey = AP_[:, dc, 0:TH], f"G1_{dc}"
                extra = [f"QA{dc}"] if dc < 16 else [f"PL{dc - 16}"]
            tt("dve", gdst, sg1[par][:], sg2[par][:], ALU.add, reads=[f"sg1{par}", f"sg2{par}"], writes=[gkey] + extra)
        P.barrier()
    G0_keys = [f"G0_{d}" for d in range(32)]
    G1_keys = [f"G1_{d}" for d in range(32)]
    d_G = dbg_out("G", [128, 2 * 32 * TH], BF16)
    if DEBUG:
        dump(d_G, G0[:].rearrange("p a b -> p (a b)"), "dbg_G0", G0_keys, 32 * TH)
        for dc_ in range(32):
            dma("sp", d_G[:, 32 * TH + dc_ * TH:32 * TH + (dc_ + 1) * TH], AP_[:, dc_, 0:TH], f"dbg_G1_{dc_ % 4}", reads=G1_keys)

    def bcast_mod(rep, repkey, col0, Y):
        for c in range(32):
            ts("dve", Y[:, c * 128:(c + 1) * 128], ident_f, modT[:, col0 + c:col0 + c + 1], None, ALU.mult, None,
               reads=["cst", "modT"], pwrites=["Ydiag"])
        for n in range(8):
            b = n % 2
            mm(ps[b][:, :], ones_f, Y[:, n * 512:(n + 1) * 512], True, True, reads=["cst", "Ydiag"], writes=[PSK[b]])
            cp("act", rep[:, n * 512:(n + 1) * 512], ps[b][:, :], reads=[PSK[b]], pwrites=[repkey])

    P.barrier()
    A.release(m5)
    g_rep = A.alloc("g_rep", [128, D], F32)
    ringo = Ring("wro", 4, [128, 16, 512])
    m5b = A.mark()
    Yd = A.alloc("Yd", [128, D], F32)
    bcast_mod(g_rep, "g_rep", 64, Yd)
    P.barrier()
    A.release(m5b)
    xo = [A.alloc("xo", [128, 512], F32) for _ in range(3)]
    to = [A.alloc("to", [128, 512], F32) for _ in range(3)]
    eo_i = [0]

    def make_evac_res(src_rows_fn, dstbuf, tb_off):
        def evac(cti, ti, tb, b):
            s = eo_i[0] % 3
            eo_i[0] += 1
            rows = slice((tb_off + tb) * 128, (tb_off + tb + 1) * 128)
            cols = slice(cti * 512, (cti + 1) * 512)
            dma("sp", xo[s][:], src_rows_fn(rows, cols), f"xo{s}", writes=[f"xo{s}"])
            tt("dve", to[s][:], ps[b][:, :], g_rep[:, cols], ALU.mult, reads=[PSK[b], "g_rep"], writes=[f"to{s}"])
            stt("dve", to[s][:], xo[s][:], ALU_ALPHA, to[s][:], ALU.mult, ALU.add, reads=[f"xo{s}", f"to{s}"], writes=[f"to{s}"])
            dma("sp", dstbuf[rows, cols], to[s][:], f"to_st{s}", reads=[f"to{s}"], pwrites=["resbuf"])
        return evac

    ALU_ALPHA = float(ALPHA)
    for th in range(2):
        if th == 0:
            lf = (lambda kc, tb: G0[:, kc, tb * 128:(tb + 1) * 128])
            lk = G0_keys
        else:
            lf = (lambda kc, tb: AP_[:, kc, tb * 128:(tb + 1) * 128])
            lk = G1_keys
        gemm_a(ringo, w_out, KC, 16, [(i * 512, 512) for i in range(8)], lf, lk, [0, 1, 2, 3],
               lambda cti, ti: 4 * (cti % 2) + ti, make_evac_res(lambda rows, cols: xs[rows, cols], ybuf, th * 4))

    if STOP_AFTER < 4:
        P.emit()
        return nc, dbg, P, A
    P.barrier()
    for th in range(2):
        A.release(mP)
        u2T = A.alloc("u2T", [128, 32, TH], BF16)
        mB = A.mark()
        lg = A.alloc("lg", [128, D], F32)
        lb = A.alloc("lb", [128, D], F32)
        yt = [A.alloc("yt", [128, D], F32) for _ in range(2)]
        x1t = [A.alloc("x1t", [128, D], F32) for _ in range(2)]
        xh6 = [A.alloc("xh6", [128, D], BF16) for _ in range(2)]
        dma("sp", lg[:], ln1g_d[0, :].partition_broadcast(128), "lg", writes=["lg"])
        dma("sp", lb[:], ln1b_d[0, :].partition_broadcast(128), "lb", writes=["lb"])
        for bi in range(4):
            blk = th * 4 + bi
            par = bi % 2
            rows = slice(blk * 128, (blk + 1) * 128)
            dma("sp", yt[par][:], ybuf[rows, :], f"yt{par}", reads=["resbuf"], writes=[f"yt{par}"])
            sm = lnsm[par]
            smk = f"ln{par}"
            ln_stats(yt[par], f"yt{par}", sm, smk)
            act(yt[par][:], yt[par][:], AF.Identity, reads=[f"yt{par}", smk + "rstd", smk + "nmr"], writes=[f"yt{par}"],
                bias=sm["nmr"][:], scale=sm["rstd"][:])
            tt("dve", x1t[par][:], yt[par][:], lg[:], ALU.mult, reads=[f"yt{par}", "lg"], writes=[f"x1t{par}"])
            tt("pool", x1t[par][:], x1t[par][:], lb[:], ALU.add, reads=[f"x1t{par}", "lb"], writes=[f"x1t{par}"])
            dma("sp", x1buf[rows, :], x1t[par][:], f"x1_st{par}", reads=[f"x1t{par}"], pwrites=["x1buf"])
            ln_to_uT(x1t[par], f"x1t{par}", xh6[par], f"xh6{par}", par,
                     (lambda c, bi=bi: u2T[:, c, bi * 128:(bi + 1) * 128]), f"u2T_b{bi}", sc2p, 96, ["modT", "sc2p"])
        u2_keys = [f"u2T_b{b}" for b in range(4)]
        P.barrier()
        A.release(mB)
        hT = A.alloc("hT", [128, HC, TH], BF16)
        mC = A.mark()
        rgu = Ring("wgu", 4, [128, 32, 256])
        sl = [A.alloc("sl", [128, TH], F32) for _ in range(2)]
        for h0 in range(0, HC, 2):
            wg, wgk = rgu.load(lambda v: v, wsrc(w_gu, 0, 32, h0 * 128, 256))
            wu, wuk = rgu.load(lambda v: v, wsrc(w_gu, 0, 32, HF + h0 * 128, 256))
            for sub in range(2):
                hc = h0 + sub
                q4 = hc % 4
                bg, bu = 2 * q4, 2 * q4 + 1
                for kc in range(32):
                    mm(ps[bg][:, :], wg[:, kc, sub * 128:(sub + 1) * 128], u2T[:, kc, :], kc == 0, kc == 31, reads=[wgk] + u2_keys,
                       writes=[PSK[bg]] if kc == 0 else (), pwrites=() if kc == 0 else [PSK[bg]])
                for kc in range(32):
                    mm(ps[bu][:, :], wu[:, kc, sub * 128:(sub + 1) * 128], u2T[:, kc, :], kc == 0, kc == 31, reads=[wuk] + u2_keys,
                       writes=[PSK[bu]] if kc == 0 else (), pwrites=() if kc == 0 else [PSK[bu]])
                sp_ = hc % 2
                act(sl[sp_][:], ps[bg][:, :], AF.Silu, reads=[PSK[bg]], writes=[f"sl{sp_}"])
                tt("dve", hT[:, hc, :], sl[sp_][:], ps[bu][:, :], ALU.mult, reads=[f"sl{sp_}", PSK[bu]], writes=[f"hT{hc}"])
        hT_keys = [f"hT{c}" for c in range(HC)]
        P.barrier()
        A.release(mP)
        g_rep2 = A.alloc("g_rep2", [128, D], F32)
        xo2 = [A.alloc("xo2", [128, 512], F32) for _ in range(3)]
        to2 = [A.alloc("to2", [128, 512], F32) for _ in range(3)]
        assert A.mark() <= mB
        A.release(mC)
        Yd2 = A.alloc("Yd2", [128, D], F32)
        bcast_mod(g_rep2, "g_rep2", 160, Yd2)
        P.barrier()
        A.release(mC)
        ringd = Ring("wrd", 4, [128, 16, 512])

        def evac_dn(cti, ti, tb, b, th=th, xo=xo2, to=to2, g_rep=g_rep2):
            s = eo_i[0] % 3
            eo_i[0] += 1
            rows = slice((th * 4 + tb) * 128, (th * 4 + tb + 1) * 128)
            cols = slice(cti * 512, (cti + 1) * 512)
            dma("sp", xo[s][:], x1buf[rows, cols], f"xo2{s}", reads=["x1buf"], writes=[f"xo2{s}"])
            tt("dve", to[s][:], ps[b][:, :], g_rep[:, cols], ALU.mult, reads=[PSK[b], "g_rep2"], writes=[f"to2{s}"])
            stt("dve", to[s][:], xo[s][:], ALU_ALPHA, to[s][:], ALU.mult, ALU.add, reads=[f"xo2{s}", f"to2{s}"], writes=[f"to2{s}"])
            dma("sp", y2buf[rows, cols], to[s][:], f"to2_st{s}", reads=[f"to2{s}"], pwrites=["y2buf"])

        gemm_a(ringd, w_dn, HC, 16, [(i * 512, 512) for i in range(8)],
               (lambda kc, tb, hT=hT: hT[:, kc, tb * 128:(tb + 1) * 128]), hT_keys, [0, 1, 2, 3],
               lambda cti, ti: 4 * (cti % 2) + ti, evac_dn)
        P.barrier()
        A.release(mP)
        lg = A.alloc("lg2", [128, D], F32)
        lb = A.alloc("lb2", [128, D], F32)
        yt = [A.alloc("yt2", [128, D], F32) for _ in range(2)]
        ot = [A.alloc("ot2", [128, D], F32) for _ in range(2)]
        dma("sp", lg[:], ln2g_d[0, :].partition_broadcast(128), "lg2", writes=["lg2"])
        dma("sp", lb[:], ln2b_d[0, :].partition_broadcast(128), "lb2", writes=["lb2"])
        for bi in range(4):
            blk = th * 4 + bi
            par = bi % 2
            rows = slice(blk * 128, (blk + 1) * 128)
            dma("sp", yt[par][:], y2buf[rows, :], f"yt2{par}", reads=["y2buf"], writes=[f"yt2{par}"])
            sm = lnsm[par]
            smk = f"ln{par}"
            ln_stats(yt[par], f"yt2{par}", sm, smk)
            act(yt[par][:], yt[par][:], AF.Identity, reads=[f"yt2{par}", smk + "rstd", smk + "nmr"], writes=[f"yt2{par}"],
                bias=sm["nmr"][:], scale=sm["rstd"][:])
            tt("dve", ot[par][:], yt[par][:], lg[:], ALU.mult, reads=[f"yt2{par}", "lg2"], writes=[f"ot2{par}"])
            tt("pool", ot[par][:], ot[par][:], lb[:], ALU.add, reads=[f"ot2{par}", "lb2"], writes=[f"ot2{par}"])
            dma("sp", out_d[rows, :], ot[par][:], f"out_st{par}", reads=[f"ot2{par}"])
        P.barrier()

    P.emit()
    return nc, dbg, P, A


def make_consts(r, c, pool_scale, b_ada, b_forget):
    cs = np.zeros((128, NCONST), np.float32)
    cs[:, CO_IDENT:CO_IDENT + 128] = np.eye(128, dtype=np.float32)
    cs[:, CO_ONES:CO_ONES + 128] = 1.0
    s = np.arange(128)
    cs[:, CO_TRI:CO_TRI + 128] = (s[:, None] <= s[None, :]).astype(np.float32)
    cs[63, CO_SEL:CO_SEL + 128] = 1.0
    gb = np.array([gblock(kb // 8, kb % 8) for kb in range(64)])
    cs[0:64, CO_L:CO_L + 64] = (gb[:, None] < gb[None, :]).astype(np.float32)
    cs[:, CO_OHR + r] = 1.0
    halo = np.ones((8, 16), np.float32)
    inv = np.zeros((4, 128), np.float32)
    for g, w in enumerate(WINDOWS):
        inv[g, :] = 1.0 / w
    for j in range(8):
        if gblock(r, j) == 0:
            halo[j, :] = 0.0
            assert j == 0
            t = np.arange(128)
            for g, w in enumerate(WINDOWS):
                inv[g, :] = 1.0 / np.minimum(t + 1, w)
    cs[:, CO_HALO:CO_HALO + 128] = halo.reshape(1, 128)
    cs[:, CO_INVC:CO_INVC + 512] = inv.reshape(1, 512)
    cs[:, CO_CT:CO_CT + 32] = c.reshape(32, 128).T
    cs[:, CO_PST:CO_PST + 32] = pool_scale.reshape(32, 128).T
    cs[:, CO_BADA:CO_BADA + 24] = b_ada[3072 * r:3072 * (r + 1)].reshape(24, 128).T
    cs[:, CO_BF:CO_BF + 16] = b_forget.reshape(1, 16)
    return cs


def make_masks(r):
    s = np.arange(128)
    tri = (s[:, None] <= s[None, :]).astype(np.float32)
    m = np.zeros((128, 2, 8, 128), np.float32)
    for rp in range(8):
        m[:, 0, rp, :] = 1.0 if rp < r else (tri if rp == r else 0.0)
        m[:, 1, rp, :] = 1.0 if rp > r else (tri if rp == r else 0.0)
    return m.reshape(128, 2 * 8 * 128)


_CACHE = {}


def kernel(x, c, w_ada, b_ada, w_in, b_forget, w_attn_out, w_pool, pool_scale, w_out, ln1_g, ln1_b,
           w_gate_up, w_down, ln2_g, ln2_b):
    f32 = np.float32
    x2 = np.asarray(x, f32).reshape(SEQ, D)
    cv = np.asarray(c, f32).reshape(D)
    w_ada2 = np.asarray(w_ada, f32).reshape(D, 6 * D)
    b_ada1 = np.asarray(b_ada, f32).reshape(6 * D)
    shared = {}
    if STOP_AFTER >= 0.6:
        w_in2 = np.asarray(w_in, f32).reshape(D, IN_COLS)
        shared["w_in_a"] = np.ascontiguousarray(w_in2[:, :C_GA])
    if STOP_AFTER >= 3:
        shared["w_in_g"] = np.ascontiguousarray(w_in2[:, C_GA:])
        shared["w_attn_out"] = np.ascontiguousarray(np.asarray(w_attn_out, f32).reshape(FW, D))
        shared["w_pool"] = np.ascontiguousarray(np.asarray(w_pool, f32).reshape(2048, 1024))
        shared["w_out"] = np.ascontiguousarray(np.asarray(w_out, f32).reshape(D, D))
    if STOP_AFTER >= 4:
        shared["w_gate_up"] = np.ascontiguousarray(np.asarray(w_gate_up, f32).reshape(D, 2 * HF))
        shared["w_down"] = np.ascontiguousarray(np.asarray(w_down, f32).reshape(HF, D))
        shared["ln1_g"] = np.asarray(ln1_g, f32).reshape(1, D)
        shared["ln1_b"] = np.asarray(ln1_b, f32).reshape(1, D)
        shared["ln2_g"] = np.asarray(ln2_g, f32).reshape(1, D)
        shared["ln2_b"] = np.asarray(ln2_b, f32).reshape(1, D)
    psc = np.asarray(pool_scale, f32).reshape(D)
    bfo = np.asarray(b_forget, f32).reshape(HEADS)
    in_maps = []
    for r in range(NCORES):
        xs = np.zeros((T + 128, D), f32)
        for j in range(NB):
            g = gblock(r, j)
            xs[j * 128:(j + 1) * 128] = x2[g * 128:(g + 1) * 128]
            if g > 0:
                xs[T + j * 16:T + (j + 1) * 16] = x2[g * 128 - 16:g * 128]
        m = dict(shared)
        m["xs"] = xs
        m["consts"] = make_consts(r, cv, psc, b_ada1, bfo)
        m["masks"] = make_masks(r)
        m["w_ada_s"] = np.ascontiguousarray(w_ada2[:, 3072 * r:3072 * (r + 1)])
        in_maps.append(m)
    if "nc" not in _CACHE:
        _CACHE["nc"] = build_program()
    nc, dbg, P, A = _CACHE["nc"]
    res = run_bass_kernel_spmd(nc, in_maps, core_ids=list(range(NCORES)))
    out = np.zeros((SEQ, D), f32)
    for r in range(NCORES):
        o = np.asarray(res.results[r]["out"], f32)
        for j in range(NB):
            g = gblock(r, j)
            out[g * 128:(g + 1) * 128] = o[j * 128:(j + 1) * 128]
    if DEBUG:
        _CACHE["dbg"] = [{k: np.asarray(res.results[r]["dbg_" + k]) for k in dbg} for r in range(NCORES)]
    return out.reshape(1, SEQ, D)
```
